# Optimizing a Trainium2 kernel written in Bass

```python
import math
import jax
import jax.numpy as jnp
from jax import lax
import numpy as np

D_MODEL = 1024
BATCH = 4
SEQ = 4096
DEPTH = 2

HEAD_DIM_ATTN = 64
N_HEADS_SB = 8
N_HEADS_DIL = 8
WIDTH_SB = N_HEADS_SB * HEAD_DIM_ATTN
WIDTH_DIL = N_HEADS_DIL * HEAD_DIM_ATTN
ATTN_WIDTH = WIDTH_SB + WIDTH_DIL
DIL_CONFIGS = ((128, 1), (512, 4), (2048, 16))
Q_BLOCK = 128
NUM_BUCKETS = 32
MAX_DISTANCE = 2048
N_HEADS_MLSTM = 8
HEAD_DIM_MLSTM = D_MODEL // N_HEADS_MLSTM
MLSTM_WIDTH = N_HEADS_MLSTM * HEAD_DIM_MLSTM
MLSTM_CHUNK = 128
CONV_WIDTH = 4
FORGET_BIAS_LO = 3.0
FORGET_BIAS_HI = 6.0
D_FF = 2816
N_EVEN = (DEPTH + 1) // 2
N_ODD = DEPTH // 2
EPS = 1e-6

kernel_name = 'hybrid_stickbreak_dilated_mlstm_macaron'


def _rms_norm(x, g):
    xf = x.astype(jnp.float32)
    y = xf * lax.rsqrt(jnp.mean(xf * xf, axis=-1, keepdims=True) + EPS)
    return (y * g.astype(jnp.float32)).astype(x.dtype)


def _swiglu(x, w_gate, w_up, w_down):
    return (jax.nn.silu(x @ w_gate) * (x @ w_up)) @ w_down


def _t5_bucket(dist):
    max_exact = NUM_BUCKETS // 2
    d = jnp.maximum(dist, 1).astype(jnp.float32)
    log_b = max_exact + (jnp.log(d / max_exact) / math.log(MAX_DISTANCE / max_exact)
                         * (NUM_BUCKETS - max_exact)).astype(jnp.int32)
    log_b = jnp.minimum(log_b, NUM_BUCKETS - 1)
    return jnp.where(dist < max_exact, dist, log_b)


def _stick_breaking_attention(q, k, v):
    bsz, t, h, dh = q.shape
    nb = t // Q_BLOCK
    qb = q.reshape(bsz, nb, Q_BLOCK, h, dh).transpose(1, 0, 2, 3, 4)
    starts = jnp.arange(nb) * Q_BLOCK
    k_pos = jnp.arange(t)
    scale = 1.0 / math.sqrt(dh)

    def block(args):
        q_blk, start = args
        z = jnp.einsum('bqhd,bkhd->bhqk', q_blk, k) * scale
        causal = k_pos[None, :] < (start + jnp.arange(Q_BLOCK))[:, None]
        log_keep = jnp.where(causal, jax.nn.log_sigmoid(-z), 0.0)
        later = lax.cumsum(log_keep, axis=3, reverse=True) - log_keep
        log_w = jnp.where(causal, jax.nn.log_sigmoid(z) + later, -jnp.inf)
        return jnp.einsum('bhqk,bkhd->bqhd', jnp.exp(log_w), v)

    out = lax.map(block, (qb, starts))
    return out.transpose(1, 0, 2, 3, 4).reshape(bsz, t, h, dh)


def _dilated_branch(q, k, v, rel_bias, window, dil):
    bsz, t, h, dh = q.shape
    steps = window // dil
    seq_l = t // dil
    nb = -(-seq_l // Q_BLOCK)
    lp = nb * Q_BLOCK
    n = bsz * dil

    def to_res(a):
        return a.reshape(bsz, seq_l, dil, h, dh).transpose(0, 2, 1, 3, 4).reshape(n, seq_l, h, dh)

    qb = jnp.pad(to_res(q), ((0, 0), (0, lp - seq_l), (0, 0), (0, 0))).reshape(n, nb, Q_BLOCK, h, dh)

    def band(a):
        a = jnp.pad(to_res(a), ((0, 0), (Q_BLOCK, lp - seq_l), (0, 0), (0, 0)))
        a = a.reshape(n, nb + 1, Q_BLOCK, h, dh)
        return jnp.concatenate([a[:, :-1], a[:, 1:]], axis=2)

    kb, vb = band(k), band(v)
    qi = jnp.arange(Q_BLOCK)[:, None]
    ki = jnp.arange(2 * Q_BLOCK)[None, :]
    dist = qi + Q_BLOCK - ki
    bias = rel_bias[_t5_bucket(jnp.maximum(dist, 0) * dil)].transpose(2, 0, 1)
    key_pos = jnp.arange(nb)[:, None, None] * Q_BLOCK + ki[None] - Q_BLOCK
    valid = (dist >= 0) & (dist <= steps) & (key_pos >= 0)
    s = jnp.einsum('nbqhd,nbkhd->nbhqk', qb, kb) / math.sqrt(dh) + bias
    s = jnp.where(valid[None, :, None], s, -jnp.inf)
    m = jnp.max(s, axis=-1)
    p = jnp.exp(s - m[..., None])
    l = jnp.sum(p, axis=-1)
    num = jnp.einsum('nbhqk,nbkhd->nbqhd', p, vb)

    def back(a):
        a = a[:, :seq_l]
        rest = a.shape[2:]
        a = a.reshape((bsz, dil, seq_l) + rest)
        a = a.transpose((0, 2, 1) + tuple(range(3, a.ndim)))
        return a.reshape((bsz, t) + rest)

    num = back(num.reshape(n, lp, h, dh))
    m = back(m.transpose(0, 1, 3, 2).reshape(n, lp, h))
    l = back(l.transpose(0, 1, 3, 2).reshape(n, lp, h))
    return num, m, l


def _dilated_attention(q, k, v, rel_bias):
    outs = [_dilated_branch(q, k, v, rel_bias, w, d) for (w, d) in DIL_CONFIGS]
    m_all = jnp.stack([o[1] for o in outs])
    wts = jnp.exp(m_all - jnp.max(m_all, axis=0))
    num = sum(wts[i][..., None] * outs[i][0] for i in range(len(outs)))
    den = sum(wts[i] * outs[i][2] for i in range(len(outs)))
    return num / den[..., None]


def _attn_mixer(xn, w_in, w_out, rel_bias):
    bsz, t, _ = xn.shape
    proj = (xn @ w_in).astype(jnp.float32)
    idx = [WIDTH_SB, 2 * WIDTH_SB, 3 * WIDTH_SB, 3 * WIDTH_SB + WIDTH_DIL, 3 * WIDTH_SB + 2 * WIDTH_DIL]
    qa, ka, va, qd, kd, vd = jnp.split(proj, idx, axis=-1)
    sb = lambda a: a.reshape(bsz, t, N_HEADS_SB, HEAD_DIM_ATTN)
    dl = lambda a: a.reshape(bsz, t, N_HEADS_DIL, HEAD_DIM_ATTN)
    out_sb = _stick_breaking_attention(sb(qa), sb(ka), sb(va))
    out_dil = _dilated_attention(dl(qd), dl(kd), dl(vd), rel_bias.astype(jnp.float32))
    mixed = jnp.concatenate([out_sb.reshape(bsz, t, WIDTH_SB), out_dil.reshape(bsz, t, WIDTH_DIL)], axis=-1)
    return mixed.astype(xn.dtype) @ w_out


def _causal_depthwise_conv(x, w):
    return lax.conv_general_dilated(
        x, w[:, None, :], window_strides=(1,), padding=((CONV_WIDTH - 1, 0),),
        dimension_numbers=('NWC', 'WIO', 'NWC'), feature_group_count=x.shape[-1])


def _mlstm(q, k, v, i_pre, f_pre):
    bsz, t, h, dh = q.shape
    nc = t // MLSTM_CHUNK
    cl = MLSTM_CHUNK

    def chunks(a):
        a = a.reshape((bsz, nc, cl, h) + a.shape[3:])
        return a.transpose((0, 3, 1, 2) + tuple(range(4, a.ndim)))

    qc = chunks(q) / math.sqrt(dh)
    kc, vc = chunks(k), chunks(v)
    ic = chunks(i_pre)
    bc = jnp.cumsum(jax.nn.log_sigmoid(chunks(f_pre)), axis=-1)

    def step(carry, xs):
        c_st, n_st, m_st = carry
        k_, v_, i_, b_ = xs
        g = b_[..., -1]
        w = g[..., None] - b_ + i_
        m_new = jnp.maximum(g + m_st, jnp.max(w, axis=-1))
        decay = jnp.exp(g + m_st - m_new)
        ws = jnp.exp(w - m_new[..., None])
        c_new = decay[..., None, None] * c_st + jnp.einsum('bhs,bhsk,bhsv->bhkv', ws, k_, v_)
        n_new = decay[..., None] * n_st + jnp.einsum('bhs,bhsk->bhk', ws, k_)
        return (c_new, n_new, m_new), (c_st, n_st, m_st)

    init = (jnp.zeros((bsz, h, dh, dh), jnp.float32),
            jnp.zeros((bsz, h, dh), jnp.float32),
            jnp.zeros((bsz, h), jnp.float32))
    xs = tuple(jnp.moveaxis(a, 2, 0) for a in (kc, vc, ic, bc))
    _, (c_prev, n_prev, m_prev) = lax.scan(step, init, xs)
    c_prev = jnp.moveaxis(c_prev, 0, 2)
    n_prev = jnp.moveaxis(n_prev, 0, 2)
    m_prev = jnp.moveaxis(m_prev, 0, 2)

    a = bc + m_prev[..., None]
    tri = jnp.arange(cl)[:, None] >= jnp.arange(cl)[None, :]
    dmat = jnp.where(tri, bc[..., :, None] - bc[..., None, :] + ic[..., None, :], -jnp.inf)
    m_t = jnp.maximum(a, jnp.max(dmat, axis=-1))
    p = jnp.einsum('bhcld,bhcsd->bhcls', qc, kc) * jnp.exp(dmat - m_t[..., None])
    w_inter = jnp.exp(a - m_t)
    h_num = (jnp.einsum('bhcls,bhcsd->bhcld', p, vc)
             + w_inter[..., None] * jnp.einsum('bhcld,bhcde->bhcle', qc, c_prev))
    n_num = jnp.sum(p, axis=-1) + w_inter * jnp.einsum('bhcld,bhcd->bhcl', qc, n_prev)
    hid = h_num / jnp.maximum(jnp.abs(n_num), jnp.exp(-m_t))[..., None]
    return hid.transpose(0, 2, 3, 1, 4).reshape(bsz, t, h, dh)


def _mlstm_mixer(xn, w_in, b_gates, conv_w, head_g, w_out):
    bsz, t, _ = xn.shape
    proj = (xn @ w_in).astype(jnp.float32)
    qk, v, o, gates = jnp.split(proj, [2 * MLSTM_WIDTH, 3 * MLSTM_WIDTH, 4 * MLSTM_WIDTH], axis=-1)
    qk = jax.nn.silu(_causal_depthwise_conv(qk, conv_w.astype(jnp.float32)))
    q, k = jnp.split(qk, 2, axis=-1)
    gates = gates + b_gates.astype(jnp.float32)
    i_pre, f_pre = gates[..., :N_HEADS_MLSTM], gates[..., N_HEADS_MLSTM:]
    hd = lambda a: a.reshape(bsz, t, N_HEADS_MLSTM, HEAD_DIM_MLSTM)
    hid = _mlstm(hd(q), hd(k), hd(v), i_pre, f_pre)
    hid = hid * lax.rsqrt(jnp.mean(hid * hid, axis=-1, keepdims=True) + EPS)
    hid = hid * head_g.astype(jnp.float32).reshape(N_HEADS_MLSTM, HEAD_DIM_MLSTM)
    hid = hid.reshape(bsz, t, MLSTM_WIDTH) * jax.nn.sigmoid(o)
    return hid.astype(xn.dtype) @ w_out


def setup_inputs(seed: int = 0) -> dict:
    key = jax.random.key(seed)
    ks = jax.random.split(key, 16)
    nrm = jax.random.normal
    x = nrm(ks[0], (BATCH, SEQ, D_MODEL), jnp.float32)
    norm_g = 1.0 + 0.02 * nrm(ks[1], (DEPTH, 6, D_MODEL), jnp.float32)
    ffn_w_gate = nrm(ks[2], (DEPTH, 2, D_MODEL, D_FF), jnp.float32) * D_MODEL ** -0.5
    ffn_w_up = nrm(ks[3], (DEPTH, 2, D_MODEL, D_FF), jnp.float32) * D_MODEL ** -0.5
    ffn_w_down = nrm(ks[4], (DEPTH, 2, D_FF, D_MODEL), jnp.float32) * D_FF ** -0.5
    attn_w_in = nrm(ks[5], (N_EVEN, D_MODEL, 3 * ATTN_WIDTH), jnp.float32) * D_MODEL ** -0.5
    attn_w_out = nrm(ks[6], (N_EVEN, ATTN_WIDTH, D_MODEL), jnp.float32) * ATTN_WIDTH ** -0.5
    rel_bias = 0.5 * nrm(ks[7], (NUM_BUCKETS, N_HEADS_DIL), jnp.float32)
    mlstm_w_in = nrm(ks[8], (N_ODD, D_MODEL, 4 * MLSTM_WIDTH + 2 * N_HEADS_MLSTM), jnp.float32) * D_MODEL ** -0.5
    b_i = 0.1 * nrm(ks[9], (N_ODD, N_HEADS_MLSTM), jnp.float32)
    b_f = (jnp.linspace(FORGET_BIAS_LO, FORGET_BIAS_HI, N_HEADS_MLSTM, dtype=jnp.float32)[None]
           + 0.1 * nrm(ks[10], (N_ODD, N_HEADS_MLSTM), jnp.float32))
    mlstm_b_gates = jnp.concatenate([b_i, b_f], axis=-1)
    mlstm_conv_w = nrm(ks[11], (N_ODD, CONV_WIDTH, 2 * MLSTM_WIDTH), jnp.float32) * CONV_WIDTH ** -0.5
    mlstm_head_g = 1.0 + 0.02 * nrm(ks[12], (N_ODD, MLSTM_WIDTH), jnp.float32)
    mlstm_w_out = nrm(ks[13], (N_ODD, MLSTM_WIDTH, D_MODEL), jnp.float32) * MLSTM_WIDTH ** -0.5
    return {'x': x, 'norm_g': norm_g, 'ffn_w_gate': ffn_w_gate, 'ffn_w_up': ffn_w_up,
            'ffn_w_down': ffn_w_down, 'attn_w_in': attn_w_in, 'attn_w_out': attn_w_out,
            'rel_bias': rel_bias, 'mlstm_w_in': mlstm_w_in, 'mlstm_b_gates': mlstm_b_gates,
            'mlstm_conv_w': mlstm_conv_w, 'mlstm_head_g': mlstm_head_g, 'mlstm_w_out': mlstm_w_out}


def reference(x, norm_g, ffn_w_gate, ffn_w_up, ffn_w_down, attn_w_in, attn_w_out, rel_bias,
              mlstm_w_in, mlstm_b_gates, mlstm_conv_w, mlstm_head_g, mlstm_w_out):
    for layer in range(DEPTH):
        g = norm_g[layer]
        j = layer // 2
        hf = _swiglu(_rms_norm(x, g[0]), ffn_w_gate[layer, 0], ffn_w_up[layer, 0], ffn_w_down[layer, 0])
        x = x + 0.5 * _rms_norm(hf, g[1])
        xn = _rms_norm(x, g[2])
        if layer % 2 == 0:
            mix = _attn_mixer(xn, attn_w_in[j], attn_w_out[j], rel_bias)
        else:
            mix = _mlstm_mixer(xn, mlstm_w_in[j], mlstm_b_gates[j], mlstm_conv_w[j],
                               mlstm_head_g[j], mlstm_w_out[j])
        x = x + _rms_norm(mix, g[3])
        hf = _swiglu(_rms_norm(x, g[4]), ffn_w_gate[layer, 1], ffn_w_up[layer, 1], ffn_w_down[layer, 1])
        x = x + 0.5 * _rms_norm(hf, g[5])
    return x
```

```python
import contextlib
import math
import numpy as np
import ml_dtypes
import concourse.bass as bass
import concourse.mybir as mybir
from concourse.bass_utils import run_bass_kernel_spmd

F32 = mybir.dt.float32
BF16 = mybir.dt.bfloat16
AF = mybir.ActivationFunctionType
ALU = mybir.AluOpType
NPBF = ml_dtypes.bfloat16

D = 1024
DFF = 2816
NF = DFF // 128
T = 4096
TOWN = 2048
NTILE = TOWN // 128
EPS = 1e-6
ENGS = ("pe", "act", "dve", "pool", "sp")


class Tile:
    def __init__(self, name, ap, st=None):
        self.name = name
        self.ap = ap
        self.st = st if st is not None else self
        self.last_write = None
        self.reads = {}
        self.dsem = None
        self.dcnt = 0

    def __getitem__(self, k):
        return self.ap[k]


class Prog:
    def __init__(self, nc, sbuf_bytes=207 * 1024):
        self.nc = nc
        self.es = contextlib.ExitStack()
        self.q = {e: [] for e in ENGS}
        self.cnt = {e: 0 for e in ENGS}
        self.sem = {}
        for e in ("pe", "act", "dve", "pool"):
            self.sem[e] = self.es.enter_context(nc.semaphore("s_" + e))
        self.waited = {e: {} for e in ENGS}
        self.nsem = 0
        self.free_dsems = []
        self.all_dma_tiles = []
        self.arena_words = sbuf_bytes // 4
        self.arena = self.es.enter_context(nc.sbuf_tensor("arena", [128, self.arena_words], F32))
        self.top = 0
        self.banks = [self.es.enter_context(nc.psum_tensor("bank%d" % i, [128, 512], F32))
                      for i in range(8)]

    def mark(self):
        return self.top

    def release(self, mark):
        self.barrier()
        self.top = mark

    def alloc(self, name, shape, dt):
        n = 1
        for s in shape[1:]:
            n *= s
        words = (n * (4 if dt == F32 else 2) + 3) // 4
        words = (words + 7) // 8 * 8
        assert self.top + words <= self.arena_words, (name, self.top, words)
        v = self.arena[0:shape[0], self.top:self.top + words]
        self.top += words
        if dt != F32:
            v = v.bitcast(dt)
        v = v[:, 0:n]
        if len(shape) == 3:
            v = v.rearrange("p (a b) -> p a b", b=shape[2])
        elif len(shape) == 4:
            v = v.rearrange("p (a b c) -> p a b c", b=shape[2], c=shape[3])
        return Tile(name, v)

    def bank(self, i, shape, dt=F32, nbanks=1):
        b = self.banks[i]
        v = b[0:shape[0], :]
        if dt != F32:
            v = v.bitcast(dt)
        n = 1
        for s in shape[1:]:
            n *= s
        v = v[:, 0:n]
        if len(shape) == 3:
            v = v.rearrange("p (a b) -> p a b", b=shape[2])
        if not hasattr(self, "bank_st"):
            self.bank_st = {}
        if i not in self.bank_st:
            self.bank_st[i] = Tile("bankst%d" % i, b[:])
        return Tile("bank%d" % i, v, st=self.bank_st[i])

    def _dsem(self, tile):
        if tile.dsem is None:
            self.nsem += 1
            tile.dsem = self.es.enter_context(self.nc.semaphore("d%d" % self.nsem))
            self.all_dma_tiles.append(tile)
        return tile.dsem

    def _deps(self, eng, reads, writes, skip_same=False):
        deps = {}
        reads = [t.st for t in reads]
        writes = [t.st for t in writes]

        def add(key, idx):
            if key == eng and skip_same:
                return
            if deps.get(key, 0) < idx:
                deps[key] = idx

        for t in reads:
            if t.last_write is not None:
                add(*t.last_write)
        for t in writes:
            if t.last_write is not None:
                add(*t.last_write)
            for k, i in t.reads.items():
                add(k, i)
        waits = []
        w = self.waited[eng]
        for key, idx in deps.items():
            if w.get(key, 0) >= idx:
                continue
            w[key] = idx
            waits.append((key, idx))
        return waits

    def op(self, eng, fn, reads=(), writes=(), signal=True):
        waits = self._deps(eng, reads, writes, eng == "pe")
        self.cnt[eng] += 1
        idx = self.cnt[eng]
        self.q[eng].append((fn, waits, idx))
        reads = [t.st for t in reads]
        writes = [t.st for t in writes]
        for t in writes:
            t.last_write = (eng, idx)
            t.reads = {}
        for t in reads:
            if t in writes:
                continue
            if t.reads.get(eng, 0) < idx:
                t.reads[eng] = idx

    def dma(self, queue, pairs, tile, write=True):
        tile = tile.st
        self._dsem(tile)
        reads = [] if write else [tile]
        writes = [tile] if write else []
        waits = self._deps(queue, reads, writes, False)
        tile.dcnt += len(pairs)
        idx = tile.dcnt
        first = True
        for (o, i) in pairs:
            def fn(e, o=o, i=i):
                return e.dma_start(out=o, in_=i)
            self.q[queue].append((fn, waits if first else [], tile))
            first = False
        if write:
            tile.last_write = (tile, idx)
            tile.reads = {}
        else:
            if tile.reads.get(tile, 0) < idx:
                tile.reads[tile] = idx

    def barrier(self):
        for e in ENGS:
            waits = []
            w = self.waited[e]
            for k in ("pe", "act", "dve", "pool"):
                if self.cnt[k] > w.get(k, 0):
                    w[k] = self.cnt[k]
                    waits.append((k, self.cnt[k]))
            for t in self.all_dma_tiles:
                if t.dcnt > w.get(t, 0):
                    w[t] = t.dcnt
                    waits.append((t, t.dcnt))
            if waits:
                self.q[e].append((None, waits, None))

    def emit(self):
        nc = self.nc
        prog = self
        need = {e: set() for e in ("pe", "act", "dve", "pool")}
        for name in ENGS:
            for fn, waits, tag in self.q[name]:
                for key, idx in waits:
                    if isinstance(key, str):
                        need[key].add(idx)
        rank = {}
        for e, sset in need.items():
            rank[e] = {sq: i + 1 for i, sq in enumerate(sorted(sset))}
        with nc.Block() as block:
            def run(e, name):
                for fn, waits, tag in prog.q[name]:
                    for key, idx in waits:
                        if isinstance(key, str):
                            e.wait_ge(prog.sem[key], rank[key][idx])
                        else:
                            e.wait_ge(key.dsem, 16 * idx)
                    if fn is None:
                        continue
                    ins = fn(e)
                    if isinstance(tag, int):
                        if tag in rank[name]:
                            ins.then_inc(prog.sem[name], 1)
                    else:
                        ins.then_inc(tag.dsem, 16)

            @block.tensor
            def _(e):
                run(e, "pe")

            @block.scalar
            def _(e):
                run(e, "act")

            @block.vector
            def _(e):
                run(e, "dve")

            @block.gpsimd
            def _(e):
                run(e, "pool")

            @block.sync
            def _(e):
                run(e, "sp")
        self.nsig = {e: len(v) for e, v in rank.items()}


class Ctx:
    pass


def load_consts(P, C, dram):
    C.ident_f = P.alloc("ident_f", [128, 128], F32)
    C.ident = P.alloc("ident", [128, 128], BF16)
    P.dma("sp", [(C.ident_f[:], dram["ident"])], C.ident_f)
    P.op("dve", lambda e: e.tensor_copy(out=C.ident[:], in_=C.ident_f[:]), [C.ident_f], [C.ident])


def load_cast_weight(P, dst, src_ap, nrows_chunks, ncols, stg, qi=[0]):
    step = stg[0].ap.shape[1]
    for k in range(nrows_chunks):
        for c0 in range(0, ncols, step):
            c1 = min(ncols, c0 + step)
            s = stg[qi[0] % len(stg)]
            P.dma("sp", [(s[:, 0:c1 - c0], src_ap[k * 128:(k + 1) * 128, c0:c1])], s)
            eng = ("act", "pool")[qi[0] % 2]
            if eng == "act":
                P.op("act", lambda e, s=s, k=k, c0=c0, c1=c1: e.copy(out=dst[:, k, c0:c1], in_=s[:, 0:c1 - c0]),
                     [s], [dst])
            else:
                P.op("pool", lambda e, s=s, k=k, c0=c0, c1=c1: e.tensor_copy(out=dst[:, k, c0:c1], in_=s[:, 0:c1 - c0]),
                     [s], [dst])
            qi[0] += 1


def rstd_from_ss(P, ss, rstd, n, scale, ss_tile=None):
    if ss_tile is None:
        ss_tile = ss
        ss_ap = ss[:, 0:n]
    else:
        ss_ap = ss
    P.op("dve", lambda e: e.tensor_scalar(out=rstd[:, 0:n], in0=ss_ap, scalar1=scale, scalar2=EPS,
                                          op0=ALU.mult, op1=ALU.add), [ss_tile], [rstd])
    P.op("act", lambda e: e.activation(out=rstd[:, 0:n], in_=rstd[:, 0:n], func=AF.Sqrt), [rstd], [rstd])
    P.op("dve", lambda e: e.reciprocal(out=rstd[:, 0:n], in_=rstd[:, 0:n]), [rstd], [rstd])


def ffn_phase(P, C, dram, wnames, gidx, x_in, x_out, mixer=None, xg_out=None, g2idx=None):
    NT = 2
    m0 = P.mark()
    wg = P.alloc("wg", [128, 8, DFF], BF16)
    wu = P.alloc("wu", [128, 8, DFF], BF16)
    wd = P.alloc("wd", [128, NF, D], BF16)
    stg = [P.alloc("stg%d" % i, [128, 512], F32) for i in range(2)]
    load_cast_weight(P, wg, dram[wnames[0]], 8, DFF, stg)
    load_cast_weight(P, wu, dram[wnames[1]], 8, DFF, stg)
    load_cast_weight(P, wd, dram[wnames[2]], NF, D, stg)
    if mixer is not None:
        mi, woname, NI, g3idx = mixer
        wo = P.alloc("wo", [128, 8, D], BF16)
        load_cast_weight(P, wo, dram[woname], 8, D, stg)
        mt = P.alloc("mt", [128, 8, 128], BF16)
        grow_m = P.alloc("grow_m", [128, D], F32)
        P.dma("sp", [(grow_m[:], dram["grow"][g3idx:g3idx + 1, :].partition_broadcast(128))], grow_m)
    gpre = C.gcol
    grow = P.alloc("grow", [128, D], F32)
    P.dma("sp", [(grow[:], dram["grow"][gidx + 1:gidx + 2, :].partition_broadcast(128))], grow)
    xts = [P.alloc("xt%d" % i, [128, D], F32) for i in range(2 * NT)]
    xn = [P.alloc("xn%d" % i, [128, D], BF16) for i in range(2)]
    xnT = P.alloc("xnT", [128, 8, NT * 128], BF16)
    actT = P.alloc("actT", [128, NF, NT * 128], BF16)
    sg = [P.alloc("sg%d" % i, [128, NT * 128], F32) for i in range(2)]
    tt = P.alloc("tt", [128, D], F32)
    ss = P.alloc("ss", [128, 4], F32)
    rstd = P.alloc("rstd", [128, 4], F32)
    ss2 = P.alloc("ss2", [128, 4], F32)
    rstd2 = P.alloc("rstd2", [128, 4], F32)
    ss3 = P.alloc("ss3", [128, 4], F32)
    rstd3 = P.alloc("rstd3", [128, 4], F32)
    if xg_out is not None:
        xn2 = [P.alloc("xn2_%d" % i, [128, D], BF16) for i in range(2)]
        xgs = [P.alloc("xgs%d" % i, [128, 8, 128], BF16) for i in range(2)]
    pT = [P.bank(0, [128, 8, 128], BF16), P.bank(1, [128, 8, 128], BF16)]
    pg = [P.bank(2, [128, 512]), P.bank(3, [128, 512])]
    pu = [P.bank(4, [128, 512]), P.bank(5, [128, 512])]
    ph = [P.bank(6, [128, 512]), P.bank(7, [128, 512])]
    gc = gidx * 8
    N = NT * 128
    ntr = 0
    import os
    STOP = int(os.environ.get("KSTOP", "99"))
    for grp in range(int(os.environ.get("KGRP", NTILE // NT)) if STOP >= 5 else (1 if STOP > 1 else 0)):
        slot0 = (grp % 2) * NT
        for i in range(NT):
            ti = grp * NT + i
            xt = xts[slot0 + i]
            P.dma("sp", [(xt[:], x_in[ti * 128:(ti + 1) * 128, :])], xt)
            if mixer is not None:
                tok = slice(ti * 128, (ti + 1) * 128)
                if NI == 8:
                    prs = []
                    for r in range(2):
                        for par in range(2):
                            prs.append((mt[64 * par:64 * par + 64, r * 4:(r + 1) * 4, :], mi[r, :, par:8:2, tok]))
                else:
                    prs = [(mt[:, r * 4:(r + 1) * 4, :], mi[r, :, :, tok]) for r in range(2)]
                P.dma("sp", prs, mt)
                for nh in range(2):
                    hp = ph[nh]
                    for ch in range(8):
                        P.op("pe", lambda e, ch=ch, nh=nh, hp=hp: e.matmul(
                            hp[:], lhsT=mt[:, ch, :], rhs=wo[:, ch, nh * 512:(nh + 1) * 512],
                            start=(ch == 0), stop=(ch == 7)), [mt, wo], [hp], signal=(ch == 7))
                    P.op("act", lambda e, hp=hp, nh=nh: e.activation(
                        out=tt[:, nh * 512:(nh + 1) * 512], in_=hp[:], func=AF.Square,
                        accum_out=ss2[:, nh:nh + 1]), [hp], [ss2, tt])
                P.op("dve", lambda e: e.tensor_tensor(out=ss2[:, 2:3], in0=ss2[:, 0:1], in1=ss2[:, 1:2], op=ALU.add),
                     [ss2], [ss2])
                rstd_from_ss(P, ss2[:, 2:3], rstd2, 1, 1.0 / D, ss_tile=ss2)
                for nh in range(2):
                    hp = ph[nh]
                    P.op("dve", lambda e, hp=hp, nh=nh: e.scalar_tensor_tensor(
                        out=tt[:, nh * 512:(nh + 1) * 512], in0=hp[:], scalar=rstd2[:, 0:1],
                        in1=grow_m[:, nh * 512:(nh + 1) * 512], op0=ALU.mult, op1=ALU.mult), [hp, rstd2, grow_m], [tt])
                P.op("pool", lambda e, xt=xt: e.tensor_tensor(out=xt[:], in0=xt[:], in1=tt[:], op=ALU.add), [xt, tt], [xt])
            xj = xn[i % 2]
            P.op("act", lambda e, xt=xt, i=i, xj=xj: e.activation(out=xj[:], in_=xt[:], func=AF.Square,
                                                                  accum_out=ss[:, i:i + 1]), [xt], [ss, xj])
        rstd_from_ss(P, ss, rstd, NT, 1.0 / D)
        for i in range(NT):
            xt = xts[slot0 + i]
            xb = xn[i % 2]
            P.op("dve", lambda e, xt=xt, xb=xb, i=i: e.tensor_scalar(
                out=xb[:], in0=xt[:], scalar1=rstd[:, i:i + 1], scalar2=None, op0=ALU.mult), [xt, rstd], [xb])
            pt = pT[ntr % 2]
            ntr += 1
            for k in range(8):
                P.op("pe", lambda e, k=k, pt=pt, xb=xb: e.transpose(out=pt[:, k, :], in_=xb[:, k * 128:(k + 1) * 128],
                                                                    identity=C.ident[:]),
                     [xb, C.ident], [pt], signal=(k == 7))
            P.op("dve", lambda e, pt=pt, i=i: e.tensor_tensor(
                out=xnT[:, :, i * 128:(i + 1) * 128], in0=pt[:],
                in1=gpre[:, gc:gc + 8].unsqueeze(2).to_broadcast([128, 8, 128]), op=ALU.mult),
                [pt, gpre], [xnT])
        if STOP < 3:
            continue
        for f in range(NF):
            g_ps = pg[f % 2]
            u_ps = pu[f % 2]
            for k in range(8):
                P.op("pe", lambda e, k=k, f=f, g_ps=g_ps: e.matmul(
                    g_ps[:, 0:N], lhsT=wg[:, k, f * 128:(f + 1) * 128], rhs=xnT[:, k, :],
                    start=(k == 0), stop=(k == 7)), [wg, xnT], [g_ps], signal=(k == 7))
            for k in range(8):
                P.op("pe", lambda e, k=k, f=f, u_ps=u_ps: e.matmul(
                    u_ps[:, 0:N], lhsT=wu[:, k, f * 128:(f + 1) * 128], rhs=xnT[:, k, :],
                    start=(k == 0), stop=(k == 7)), [wu, xnT], [u_ps], signal=(k == 7))
            s = sg[f % 2]
            P.op("act", lambda e, g_ps=g_ps, s=s: e.activation(out=s[:], in_=g_ps[:, 0:N], func=AF.Silu),
                 [g_ps], [s])
            P.op("dve", lambda e, u_ps=u_ps, s=s, f=f: e.tensor_tensor(
                out=actT[:, f, :], in0=s[:], in1=u_ps[:, 0:N], op=ALU.mult), [s, u_ps], [actT])
        if STOP < 4:
            continue
        for i in range(NT):
            ti = grp * NT + i
            xt = xts[slot0 + i]
            for nh in range(2):
                hp = ph[nh]
                for f in range(NF):
                    P.op("pe", lambda e, f=f, i=i, nh=nh, hp=hp: e.matmul(
                        hp[:], lhsT=actT[:, f, i * 128:(i + 1) * 128], rhs=wd[:, f, nh * 512:(nh + 1) * 512],
                        start=(f == 0), stop=(f == NF - 1)), [actT, wd], [hp], signal=(f == NF - 1))
                P.op("act", lambda e, hp=hp, nh=nh: e.activation(
                    out=tt[:, nh * 512:(nh + 1) * 512], in_=hp[:], func=AF.Square,
                    accum_out=ss2[:, nh:nh + 1]), [hp], [ss2, tt])
            P.op("dve", lambda e: e.tensor_tensor(out=ss2[:, 2:3], in0=ss2[:, 0:1], in1=ss2[:, 1:2], op=ALU.add),
                 [ss2], [ss2])
            P.op("dve", lambda e: e.tensor_scalar(out=rstd2[:, 0:1], in0=ss2[:, 2:3], scalar1=1.0 / D, scalar2=EPS,
                                                  op0=ALU.mult, op1=ALU.add), [ss2], [rstd2])
            P.op("act", lambda e: e.activation(out=rstd2[:, 0:1], in_=rstd2[:, 0:1], func=AF.Sqrt), [rstd2], [rstd2])
            P.op("dve", lambda e: e.reciprocal(out=rstd2[:, 1:2], in_=rstd2[:, 0:1]), [rstd2], [rstd2])
            P.op("dve", lambda e: e.tensor_scalar(out=rstd2[:, 2:3], in0=rstd2[:, 1:2], scalar1=0.5, scalar2=None,
                                                  op0=ALU.mult), [rstd2], [rstd2])
            for nh in range(2):
                hp = ph[nh]
                P.op("dve", lambda e, hp=hp, nh=nh: e.scalar_tensor_tensor(
                    out=tt[:, nh * 512:(nh + 1) * 512], in0=hp[:], scalar=rstd2[:, 2:3],
                    in1=grow[:, nh * 512:(nh + 1) * 512], op0=ALU.mult, op1=ALU.mult), [hp, rstd2, grow], [tt])
            P.op("pool", lambda e, xt=xt: e.tensor_tensor(out=xt[:], in0=xt[:], in1=tt[:], op=ALU.add), [xt, tt], [xt])
            P.dma("sp", [(x_out[ti * 128:(ti + 1) * 128, :], xt[:])], xt, write=False)
            if xg_out is not None:
                g2c = g2idx * 8
                xb = xn2[ti % 2]
                P.op("act", lambda e, xt=xt, xb=xb: e.activation(out=xb[:], in_=xt[:], func=AF.Square,
                                                                 accum_out=ss3[:, 0:1]), [xt], [ss3, xb])
                rstd_from_ss(P, ss3, rstd3, 1, 1.0 / D)
                P.op("dve", lambda e, xt=xt, xb=xb: e.tensor_scalar(
                    out=xb[:], in0=xt[:], scalar1=rstd3[:, 0:1], scalar2=None, op0=ALU.mult), [xt, rstd3], [xb])
                pt = pT[ntr % 2]
                ntr += 1
                for k in range(8):
                    P.op("pe", lambda e, k=k, pt=pt, xb=xb: e.transpose(
                        out=pt[:, k, :], in_=xb[:, k * 128:(k + 1) * 128], identity=C.ident[:]),
                        [xb, C.ident], [pt], signal=(k == 7))
                xs = xgs[ti % 2]
                P.op("dve", lambda e, pt=pt, xs=xs: e.tensor_tensor(
                    out=xs[:], in0=pt[:], in1=gpre[:, g2c:g2c + 8].unsqueeze(2).to_broadcast([128, 8, 128]),
                    op=ALU.mult), [pt, gpre], [xs])
                P.dma("sp", [(xg_out[:, :, ti * 128:(ti + 1) * 128], xs[:])], xs, write=False)
    P.release(m0)


def common_inputs(nc, dram, names):
    for n, (shape, dt) in names.items():
        dram[n] = nc.dram_tensor(n, list(shape), dt, kind="ExternalInput").ap()


def build_phaseA():
    nc = bass.Bass("TRN2", target_bir_lowering=False)
    dram = {}
    common_inputs(nc, dram, {
        "ident": ([128, 128], F32), "gcol": ([128, 96], F32), "grow": ([12, D], F32),
        "x": ([TOWN, D], F32), "wg": ([D + 1, DFF], F32), "wu": ([D + 1, DFF], F32), "wd": ([DFF + 1, D], F32),
    })
    dram["xo"] = nc.dram_tensor("xo", [TOWN, D], F32, kind="ExternalOutput").ap()
    dram["xg"] = nc.dram_tensor("xg", [128, 8, TOWN], BF16, kind="ExternalOutput").ap()
    P = Prog(nc)
    C = Ctx()
    load_consts(P, C, dram)
    C.gcol = P.alloc("gcol", [128, 96], F32)
    P.dma("sp", [(C.gcol[:], dram["gcol"])], C.gcol)
    ffn_phase(P, C, dram, ("wg", "wu", "wd"), 0, dram["x"], dram["xo"], xg_out=dram["xg"], g2idx=2)
    P.barrier()
    P.emit()
    return nc


def host_consts(norm_g):
    ident = np.eye(128, dtype=np.float32)
    g = np.asarray(norm_g, np.float32).reshape(12, D)
    gcol = np.ascontiguousarray(g.reshape(12, 8, 128).transpose(2, 0, 1).reshape(128, 96))
    return ident, gcol, np.ascontiguousarray(g)


DIL = (1, 4, 16)
NEG = -30000.0


def attn_phase(P, C, dram):
    xg = dram["xg"]
    mo = dram["mo"]
    m0 = P.mark()
    dmask = P.alloc("dmask", [128, 128], F32)
    P.dma("sp", [(dmask[:], dram["dmask"])], dmask)
    trif = P.alloc("trif", [128, 128], F32)
    P.dma("sp", [(trif[:], dram["tri"])], trif)
    trib = P.alloc("trib", [128, 128], BF16)
    P.op("dve", lambda e: e.tensor_copy(out=trib[:], in_=trif[:]), [trif], [trib])
    onesb = P.alloc("onesb", [128, 128], BF16)
    P.op("pool", lambda e: e.memset(onesb[:], 1.0), [], [onesb])
    onec = P.alloc("onec", [128, 1], F32)
    P.op("pool", lambda e: e.memset(onec[:], 1.0), [], [onec])
    kTd = P.alloc("kTd", [128, 2, T], BF16)
    qde = P.alloc("qde", [128, 2, T], BF16)
    qdo = P.alloc("qdo", [128, 2, T], BF16)
    vd = P.alloc("vd", [128, 32, 256], BF16)
    m1 = P.mark()
    kTs = P.alloc("kTs", [128, 2, T], BF16)
    qse = P.alloc("qse", [128, 2, T], BF16)
    qso = P.alloc("qso", [128, 2, T], BF16)
    vs = P.alloc("vs", [128, 32, 256], BF16)
    for qt, lo in ((qde, 64), (qdo, 0), (qse, 64), (qso, 0)):
        P.op("pool", lambda e, qt=qt, lo=lo: e.memset(qt[lo:lo + 64, :, :], 0.0), [], [qt])
    m2 = P.mark()
    w = P.alloc("w", [128, 8, 1536], BF16)
    stg = [P.alloc("stgB%d" % i, [128, 512], F32) for i in range(2)]
    load_cast_weight(P, w, dram["wqkv"], 8, 1536, stg)
    xgs = [P.alloc("xgs%d" % i, [128, 8, 512], BF16) for i in range(2)]
    pb = [P.bank(i, [128, 512]) for i in range(4)]
    nps = 0
    fm_dst = [(qse, qso, 0, 0.125), (qse, qso, 1, 0.125), (kTs, kTs, 0, 1.0), (kTs, kTs, 1, 1.0),
              (qde, qdo, 0, 0.125), (qde, qdo, 1, 0.125), (kTd, kTd, 0, 1.0), (kTd, kTd, 1, 1.0)]
    for tg in range(8):
        xs = xgs[tg % 2]
        P.dma("sp", [(xs[:], xg[tg // 4, :, :, (tg % 4) * 512:(tg % 4 + 1) * 512])], xs)
        for ci, (dse, dso, ch, sc) in enumerate(fm_dst):
            col0 = ci * 128
            ps = pb[nps % 4]
            nps += 1
            for k in range(8):
                P.op("pe", lambda e, k=k, ps=ps, col0=col0, xs=xs: e.matmul(
                    ps[:], lhsT=w[:, k, col0:col0 + 128], rhs=xs[:, k, :], start=(k == 0), stop=(k == 7)),
                    [w, xs], [ps])
            cs = slice(tg * 512, (tg + 1) * 512)
            if ci % 2 == 0:
                P.op("act", lambda e, dse=dse, ch=ch, ps=ps, cs=cs, sc=sc: e.activation(
                    out=dse[0:64, ch, cs], in_=ps[0:64, :], func=AF.Copy, scale=sc), [ps], [dse])
                P.op("act", lambda e, dso=dso, ch=ch, ps=ps, cs=cs, sc=sc: e.activation(
                    out=dso[64:128, ch, cs], in_=ps[64:128, :], func=AF.Copy, scale=sc), [ps], [dso])
            else:
                P.op("dve", lambda e, dse=dse, ch=ch, ps=ps, cs=cs, sc=sc: e.tensor_scalar(
                    out=dse[0:64, ch, cs], in0=ps[0:64, :], scalar1=sc, scalar2=None, op0=ALU.mult), [ps], [dse])
                P.op("dve", lambda e, dso=dso, ch=ch, ps=ps, cs=cs, sc=sc: e.tensor_scalar(
                    out=dso[64:128, ch, cs], in0=ps[64:128, :], scalar1=sc, scalar2=None, op0=ALU.mult), [ps], [dso])
        for t in range(4):
            ti = tg * 4 + t
            ps = pb[nps % 4]
            nps += 1
            for k in range(8):
                P.op("pe", lambda e, k=k, ps=ps, xs=xs, t=t: e.matmul(
                    ps[:], lhsT=xs[:, k, t * 128:(t + 1) * 128], rhs=w[:, k, 1024:1536], start=(k == 0), stop=(k == 7)),
                    [w, xs], [ps])
            if ti % 2 == 0:
                P.op("act", lambda e, ps=ps, ti=ti: e.copy(out=vs[:, ti, :], in_=ps[:, 0:256]), [ps], [vs])
                P.op("act", lambda e, ps=ps, ti=ti: e.copy(out=vd[:, ti, :], in_=ps[:, 256:512]), [ps], [vd])
            else:
                P.op("dve", lambda e, ps=ps, ti=ti: e.tensor_copy(out=vs[:, ti, :], in_=ps[:, 0:256]), [ps], [vs])
                P.op("dve", lambda e, ps=ps, ti=ti: e.tensor_copy(out=vd[:, ti, :], in_=ps[:, 256:512]), [ps], [vd])
    P.release(m2)
    Sb = [P.bank(0, [128, 512]), P.bank(1, [128, 512])]
    cumb = [P.bank(2, [128, 512]), P.bank(3, [128, 512])]
    pvb = [P.bank(4, [64, 512]), P.bank(5, [64, 512])]
    eb = [P.alloc("e%d" % i, [128, 512], F32) for i in range(2)]
    spt = [P.alloc("sp%d" % i, [128, 512], F32) for i in range(2)]
    spm = [P.alloc("spm%d" % i, [128, 512], BF16) for i in range(2)]
    e2 = [P.alloc("e2%d" % i, [128, 512], F32) for i in range(2)]
    wtmp = P.alloc("wtmp", [128, 512], F32)
    Wb = [P.alloc("Wb%d" % i, [128, 512], BF16) for i in range(2)]
    spacc = P.alloc("spacc", [128, 512], F32)
    spaccb = [P.alloc("spaccb%d" % i, [128, 512], BF16) for i in range(2)]
    ost = [P.alloc("ost%d" % i, [64, 4, 128], BF16) for i in range(2)]
    dm4 = dmask[:].unsqueeze(1).to_broadcast([128, 4, 128])
    u = 0
    import os
    NQB = int(os.environ.get("KQB", "32"))
    for qb in range(NQB):
        pv = pvb[qb % 2]
        nacc = 0
        qs = slice(qb * 128, (qb + 1) * 128)
        for kb in range(qb, -1, -1):
            S = Sb[u % 2]
            cum = cumb[u % 2]
            e_ = eb[u % 2]
            sp_ = spt[u % 2]
            sm = spm[u % 2]
            E2 = e2[u % 2]
            W = Wb[u % 2]
            u += 1
            diag = (kb == qb)
            ks = slice(kb * 128, (kb + 1) * 128)
            for h in range(4):
                hc, hp = h // 2, h % 2
                qsrc = qso if hp else qse
                P.op("pe", lambda e, S=S, h=h, hc=hc, ks=ks, qs=qs, qsrc=qsrc: e.matmul(
                    S[:, h * 128:(h + 1) * 128], lhsT=kTs[:, hc, ks], rhs=qsrc[:, hc, qs], start=True, stop=True),
                    [kTs, qsrc], [S])
            P.op("act", lambda e, S=S, e_=e_: e.activation(out=e_[:], in_=S[:], func=AF.Exp), [S], [e_])
            P.op("act", lambda e, sp_=sp_, e_=e_: e.activation(out=sp_[:], in_=e_[:], func=AF.Ln, bias=onec[:, 0:1]),
                 [e_, onec], [sp_])
            if diag:
                P.op("dve", lambda e, sm=sm, sp_=sp_: e.tensor_tensor(
                    out=sm[:].rearrange("p (h q) -> p h q", h=4), in0=sp_[:].rearrange("p (h q) -> p h q", h=4),
                    in1=dm4, op=ALU.mult), [sp_, dmask], [sm])
            else:
                P.op("dve", lambda e, sm=sm, sp_=sp_: e.tensor_copy(out=sm[:], in_=sp_[:]), [sp_], [sm])
            P.op("pe", lambda e, cum=cum, sm=sm, diag=diag: e.matmul(
                cum[:], lhsT=trib[:], rhs=sm[:], start=True, stop=diag), [trib, sm], [cum])
            if not diag:
                sab = spaccb[nacc % 2]
                P.op("pe", lambda e, cum=cum, sab=sab: e.matmul(
                    cum[:], lhsT=onesb[:], rhs=sab[:], start=False, stop=True), [onesb, sab], [cum])
            P.op("act", lambda e, E2=E2, cum=cum: e.activation(out=E2[:], in_=cum[:], func=AF.Exp, scale=-1.0),
                 [cum], [E2])
            if diag:
                P.op("dve", lambda e, e_=e_, E2=E2: e.tensor_tensor(out=wtmp[:], in0=e_[:], in1=E2[:], op=ALU.mult),
                     [e_, E2], [wtmp])
                P.op("dve", lambda e, W=W: e.tensor_tensor(
                    out=W[:].rearrange("p (h q) -> p h q", h=4), in0=wtmp[:].rearrange("p (h q) -> p h q", h=4),
                    in1=dm4, op=ALU.mult), [wtmp, dmask], [W])
            else:
                P.op("dve", lambda e, e_=e_, E2=E2, W=W: e.tensor_tensor(out=W[:], in0=e_[:], in1=E2[:], op=ALU.mult),
                     [e_, E2], [W])
            if kb > 0:
                if diag:
                    P.op("pool", lambda e, sp_=sp_: e.tensor_tensor(
                        out=spacc[:].rearrange("p (h q) -> p h q", h=4),
                        in0=sp_[:].rearrange("p (h q) -> p h q", h=4), in1=dm4, op=ALU.mult),
                        [sp_, dmask], [spacc])
                else:
                    P.op("pool", lambda e, sp_=sp_: e.tensor_tensor(out=spacc[:], in0=spacc[:], in1=sp_[:], op=ALU.add),
                         [sp_, spacc], [spacc])
                nacc += 1
                sab = spaccb[nacc % 2]
                P.op("pool", lambda e, sab=sab: e.tensor_copy(out=sab[:], in_=spacc[:]), [spacc], [sab])
            for h in range(4):
                P.op("pe", lambda e, pv=pv, h=h, kb=kb, W=W, diag=diag: e.matmul(
                    pv[:, h * 128:(h + 1) * 128], lhsT=vs[:, kb, h * 64:(h + 1) * 64], rhs=W[:, h * 128:(h + 1) * 128],
                    start=(diag and h == 0), stop=(kb == 0 and h == 3)), [vs, W], [pv])
        o = ost[qb % 2]
        P.op("act", lambda e, o=o, pv=pv: e.copy(out=o[:].rearrange("p h q -> p (h q)"), in_=pv[:]), [pv], [o])
        P.dma("sp", [(mo[qb // 16, :, 0:4, (qb % 16) * 128:(qb % 16 + 1) * 128], o[:])], o, write=False)
    P.release(m1)
    NJ = 17
    btab = P.alloc("btab", [128, NJ, 512], F32)
    bstg = [P.alloc("bstg%d" % i, [128, 512], F32) for i in range(2)]
    nb_ = 0
    for j in range(NJ):
        for c in (2, 1, 0):
            if j > DIL[c]:
                continue
            bs = bstg[nb_ % 2]
            nb_ += 1
            P.dma("sp", [(bs[:], dram["bt"][c, j])], bs)
            if c == 2:
                P.op("act", lambda e, bs=bs, j=j: e.activation(out=btab[:, j, :], in_=bs[:], func=AF.Exp), [bs], [btab])
            else:
                P.op("act", lambda e, bs=bs: e.activation(out=bs[:], in_=bs[:], func=AF.Exp), [bs], [bs])
                P.op("dve", lambda e, bs=bs, j=j: e.tensor_tensor(out=btab[:, j, :], in0=btab[:, j, :], in1=bs[:],
                                                                  op=ALU.add), [bs, btab], [btab])
    Eb = [P.alloc("Eb%d" % i, [128, 512], F32) for i in range(2)]
    Ebb = [P.alloc("Ebb%d" % i, [128, 512], BF16) for i in range(2)]
    rl = [P.alloc("rl%d" % i, [64, 512], F32) for i in range(2)]
    ost2 = [P.alloc("ostd%d" % i, [64, 4, 128], BF16) for i in range(2)]
    Sb = [P.bank(0, [128, 512]), P.bank(1, [128, 512])]
    pnb = [P.bank(2, [64, 512]), P.bank(3, [64, 512])]
    plb = [P.bank(4, [64, 512]), P.bank(5, [64, 512])]
    u = 0
    NDB = int(os.environ.get("KDB", "32"))
    for qb in range(NDB):
        pn = pnb[qb % 2]
        pl = plb[qb % 2]
        qs = slice(qb * 128, (qb + 1) * 128)
        js = [j for j in range(NJ) if qb - j >= 0]
        for ji, j in enumerate(js):
            kb = qb - j
            ks = slice(kb * 128, (kb + 1) * 128)
            S = Sb[u % 2]
            E = Eb[u % 2]
            Eh = Ebb[u % 2]
            u += 1
            for h in range(4):
                hc, hp = h // 2, h % 2
                qsrc = qdo if hp else qde
                P.op("pe", lambda e, S=S, h=h, hc=hc, ks=ks, qs=qs, qsrc=qsrc: e.matmul(
                    S[:, h * 128:(h + 1) * 128], lhsT=kTd[:, hc, ks], rhs=qsrc[:, hc, qs], start=True, stop=True),
                    [kTd, qsrc], [S])
            P.op("act", lambda e, S=S, E=E: e.activation(out=E[:], in_=S[:], func=AF.Exp), [S], [E])
            P.op("dve", lambda e, E=E, Eh=Eh, j=j: e.tensor_tensor(out=Eh[:], in0=E[:], in1=btab[:, j, :], op=ALU.mult),
                 [E, btab], [Eh])
            first = (ji == 0)
            last = (ji == len(js) - 1)
            for h in range(4):
                P.op("pe", lambda e, pn=pn, h=h, kb=kb, Eh=Eh, first=first, last=last: e.matmul(
                    pn[:, h * 128:(h + 1) * 128], lhsT=vd[:, kb, h * 64:(h + 1) * 64],
                    rhs=Eh[:, h * 128:(h + 1) * 128], start=(first and h == 0), stop=(last and h == 3)),
                    [vd, Eh], [pn])
            P.op("pe", lambda e, pl=pl, Eh=Eh, first=first, last=last: e.matmul(
                pl[:], lhsT=onesb[:, 0:64], rhs=Eh[:], start=first, stop=last), [onesb, Eh], [pl])
        r_ = rl[qb % 2]
        o = ost2[qb % 2]
        P.op("dve", lambda e, r_=r_, pl=pl: e.reciprocal(out=r_[:], in_=pl[:]), [pl], [r_])
        P.op("dve", lambda e, r_=r_, pn=pn, o=o: e.tensor_tensor(
            out=o[:].rearrange("p h q -> p (h q)"), in0=pn[:], in1=r_[:], op=ALU.mult), [pn, r_], [o])
        P.dma("sp", [(mo[qb // 16, :, 4:8, (qb % 16) * 128:(qb % 16 + 1) * 128], o[:])], o, write=False)
    P.release(m0)


def build_phaseB():
    nc = bass.Bass("TRN2", target_bir_lowering=False)
    dram = {}
    common_inputs(nc, dram, {
        "ident": ([128, 128], F32), "dmask": ([128, 128], F32), "tri": ([128, 128], F32),
        "xg": ([2, 128, 8, TOWN], BF16), "wqkv": ([D, 1536], F32), "bt": ([3, 17, 128, 512], F32),
    })
    dram["mo"] = nc.dram_tensor("mo", [2, 64, 8, TOWN], BF16, kind="ExternalOutput").ap()
    P = Prog(nc)
    C = Ctx()
    load_consts(P, C, dram)
    attn_phase(P, C, dram)
    P.barrier()
    P.emit()
    return nc


def t5_bucket(dist):
    dist = np.asarray(dist, np.int64)
    dd = np.maximum(dist, 1).astype(np.float32)
    log_b = 16 + (np.log(dd / np.float32(16)) / np.float32(math.log(2048 / 16)) * np.float32(16)).astype(np.int32)
    log_b = np.minimum(log_b, 31)
    return np.where(dist < 16, dist, log_b)


def host_attn_consts(rel_bias, heads):
    a = np.arange(128)
    dmask = (a[:, None] < a[None, :]).astype(np.float32)
    tri = (a[:, None] >= a[None, :]).astype(np.float32)
    rb = np.asarray(rel_bias, np.float32)
    bt = np.full((3, 17, 128, 4, 128), NEG, np.float32)
    for c, d in enumerate(DIL):
        for j in range(17):
            delta = 128 * j + a[None, :] - a[:, None]
            valid = (delta >= 0) & (delta % d == 0) & (delta <= 128 * d)
            bk = t5_bucket(np.maximum(delta, 0))
            for hi, h in enumerate(heads):
                bt[c, j, :, hi, :] = np.where(valid, rb[bk, h], NEG)
    return dmask, tri, np.ascontiguousarray(bt.reshape(3, 17, 128, 512))


def host_wqkv(attn_w_in, p):
    w = np.asarray(attn_w_in, np.float32)
    cols = []
    for base in (0, 512, 1536, 2048, 1024, 2560):
        cols.append(w[:, base + p * 256: base + (p + 1) * 256])
    return np.ascontiguousarray(np.concatenate(cols, axis=1))


def mlstm_phase(P, C, dram):
    xg = dram["xg"]
    mo = dram["mo2"]
    m0 = P.mark()
    QS = 1.0 / math.sqrt(128.0)
    mle = P.alloc("mle", [128, 128], F32)
    P.dma("sp", [(mle[:], dram["mle"])], mle)
    onesf = P.alloc("onesf", [128, 128], F32)
    P.op("pool", lambda e: e.memset(onesf[:], 1.0), [], [onesf])
    onec = P.alloc("onec", [128, 1], F32)
    P.op("pool", lambda e: e.memset(onec[:], 1.0), [], [onec])
    epsc = P.alloc("epsc", [128, 1], F32)
    P.op("pool", lambda e: e.memset(epsc[:], EPS), [], [epsc])
    cw = P.alloc("cw", [128, 4, 2, 4], F32)
    P.dma("sp", [(cw[:], dram["cw"])], cw)
    hg = P.alloc("hg", [128, 512], F32)
    P.dma("sp", [(hg[:], dram["hg"].partition_broadcast(128))], hg)
    bg = P.alloc("bg", [128, 8], F32)
    P.dma("sp", [(bg[:], dram["bg"].partition_broadcast(128))], bg)
    xgT = P.alloc("xgT", [128, 8, T], BF16)
    P.dma("sp", [(xgT[:, :, r * 2048:(r + 1) * 2048], xg[r]) for r in range(2)], xgT)
    w = P.alloc("wm", [128, 8, 2048], BF16)
    stg = [P.alloc("stgD%d" % i, [128, 512], F32) for i in range(2)]
    load_cast_weight(P, w, dram["wm"], 8, 2048, stg)
    wgf = P.alloc("wgf", [128, 8, 8], F32)
    P.dma("sp", [(wgf[:], dram["wgt"].rearrange("(k p) n -> p k n", p=128))], wgf)
    wgb = P.alloc("wgb", [128, 8, 8], BF16)
    P.op("dve", lambda e: e.tensor_copy(out=wgb[:], in_=wgf[:]), [wgf], [wgb])
    G = P.alloc("G", [128, 32, 8], F32)
    spf = P.alloc("spf", [128, 32, 4], F32)
    ef = P.alloc("ef", [128, 32, 4], F32)
    ut = P.alloc("ut", [128, 32, 4], F32)
    eu = P.alloc("eu", [128, 32, 4], F32)
    sc = P.alloc("sc", [128, 32, 4], F32)
    eg = P.alloc("eg", [128, 32, 4], F32)
    gps = P.bank(0, [128, 32, 8])
    for c in range(32):
        for k in range(8):
            P.op("pe", lambda e, c=c, k=k: e.matmul(
                gps[:, c, :], lhsT=xgT[:, k, c * 128:(c + 1) * 128], rhs=wgb[:, k, :],
                start=(k == 0), stop=(k == 7)), [xgT, wgb], [gps], signal=(c == 31 and k == 7))
    P.op("dve", lambda e: e.tensor_tensor(out=G[:], in0=gps[:], in1=bg[:].unsqueeze(1).to_broadcast([128, 32, 8]),
                                          op=ALU.add), [gps, bg], [G])
    P.op("act", lambda e: e.activation(out=ef[:], in_=G[:, :, 4:8], func=AF.Exp, scale=-1.0), [G], [ef])
    P.op("act", lambda e: e.activation(out=spf[:], in_=ef[:], func=AF.Ln, bias=onec[:, 0:1]), [ef, onec], [spf])
    bps = P.bank(1, [128, 32, 4])
    gsp = P.bank(2, [128, 32, 4])
    spf2 = spf[:].rearrange("p c h -> p (c h)")
    P.op("pe", lambda e: e.matmul(bps[:].rearrange("p c h -> p (c h)"), lhsT=mle[:], rhs=spf2, start=True, stop=True),
         [mle, spf], [bps])
    P.op("pe", lambda e: e.matmul(gsp[:].rearrange("p c h -> p (c h)"), lhsT=onesf[:], rhs=spf2, start=True, stop=True),
         [onesf, spf], [gsp])
    P.op("dve", lambda e: e.tensor_tensor(out=ut[:], in0=bps[:], in1=G[:, :, 0:4], op=ALU.add), [bps, G], [ut])
    P.op("act", lambda e: e.activation(out=eu[:], in_=ut[:], func=AF.Exp), [ut], [eu])
    P.op("act", lambda e: e.activation(out=sc[:], in_=bps[:], func=AF.Exp, scale=-1.0), [bps, ut], [sc])
    P.op("dve", lambda e: e.tensor_scalar(out=sc[:], in0=sc[:], scalar1=QS, scalar2=None, op0=ALU.mult), [sc], [sc])
    P.op("act", lambda e: e.activation(out=eg[:], in_=gsp[:], func=AF.Exp, scale=-1.0), [gsp], [eg])
    qpre = P.alloc("qpre", [128, T + 8], F32)
    qc = P.alloc("qc", [128, T], F32)
    qT = P.alloc("qT", [128, T], BF16)
    kT = P.alloc("kT", [128, T], BF16)
    vaug = P.alloc("vaug", [128, 32, 129], BF16)
    og = P.alloc("og", [128, 32, 128], BF16)
    ogt = [P.alloc("ogt%d" % i, [128, 512], F32) for i in range(2)]
    kse = [P.alloc("kse%d" % i, [128, 128], BF16) for i in range(2)]
    pT = [P.alloc("pT%d" % i, [128, 128], BF16) for i in range(2)]
    Bst = P.alloc("Bst", [128, 129], F32)
    Ab = [P.alloc("Ab%d" % i, [128, 129], BF16) for i in range(2)]
    sm = [P.alloc("sm%d" % i, [128, 8], F32) for i in range(2)]
    hid = [P.alloc("hid%d" % i, [128, 128], F32) for i in range(2)]
    hob = [P.alloc("hob%d" % i, [128, 128], BF16) for i in range(2)]
    hst = [P.alloc("hst%d" % i, [128, 128], BF16) for i in range(2)]
    P.op("pool", lambda e: e.memset(qpre[:, 0:3], 0.0), [], [qpre])
    P.op("pool", lambda e: e.memset(vaug[:, :, 128:129], 1.0), [], [vaug])
    pb = [P.bank(i, [128, 512]) for i in range(4)]
    ktb = [P.bank(4, [128, 128], BF16), P.bank(5, [128, 128], BF16)]
    hab = [P.bank(6, [128, 129]), P.bank(7, [128, 129])]
    nps = 0
    import os
    NH = int(os.environ.get("KNH", "4"))
    NC_ = int(os.environ.get("KNC", "32"))
    for h in range(NH):
        for qk, dstT in ((0, qT), (1, kT)):
            for tg in range(8):
                ps = pb[nps % 4]
                nps += 1
                col0 = qk * 512 + h * 128
                for k in range(8):
                    P.op("pe", lambda e, k=k, ps=ps, col0=col0, tg=tg: e.matmul(
                        ps[:], lhsT=w[:, k, col0:col0 + 128], rhs=xgT[:, k, tg * 512:(tg + 1) * 512],
                        start=(k == 0), stop=(k == 7)), [w, xgT], [ps], signal=(k == 7))
                if tg % 2 == 0:
                    P.op("act", lambda e, ps=ps, tg=tg: e.copy(out=qpre[:, 3 + tg * 512:3 + (tg + 1) * 512], in_=ps[:]),
                         [ps], [qpre])
                else:
                    P.op("dve", lambda e, ps=ps, tg=tg: e.tensor_copy(out=qpre[:, 3 + tg * 512:3 + (tg + 1) * 512],
                                                                      in_=ps[:]), [ps], [qpre])
            P.op("dve", lambda e, h=h, qk=qk: e.tensor_scalar(
                out=qc[:], in0=qpre[:, 0:T], scalar1=cw[:, h, qk, 0:1], scalar2=None, op0=ALU.mult), [qpre, cw], [qc])
            for j in range(1, 4):
                P.op("dve", lambda e, h=h, qk=qk, j=j: e.scalar_tensor_tensor(
                    out=qc[:], in0=qpre[:, j:j + T], scalar=cw[:, h, qk, j:j + 1], in1=qc[:],
                    op0=ALU.mult, op1=ALU.add), [qpre, cw, qc], [qc])
            P.op("act", lambda e, dstT=dstT: e.activation(out=dstT[:], in_=qc[:], func=AF.Silu), [qc], [dstT])
        for c4 in range(8):
            ps = pb[nps % 4]
            nps += 1
            for cc in range(4):
                c = c4 * 4 + cc
                for k in range(8):
                    P.op("pe", lambda e, k=k, ps=ps, c=c, cc=cc, h=h: e.matmul(
                        ps[:, cc * 128:(cc + 1) * 128], lhsT=xgT[:, k, c * 128:(c + 1) * 128],
                        rhs=w[:, k, 1024 + h * 128:1024 + (h + 1) * 128], start=(k == 0), stop=(k == 7)),
                        [w, xgT], [ps], signal=(cc == 3 and k == 7))
            P.op("act", lambda e, ps=ps, c4=c4: e.copy(
                out=vaug[:, c4 * 4:(c4 + 1) * 4, 0:128], in_=ps[:].rearrange("p (a b) -> p a b", a=4)), [ps], [vaug])
            ps = pb[nps % 4]
            nps += 1
            for cc in range(4):
                c = c4 * 4 + cc
                for k in range(8):
                    P.op("pe", lambda e, k=k, ps=ps, c=c, cc=cc, h=h: e.matmul(
                        ps[:, cc * 128:(cc + 1) * 128], lhsT=xgT[:, k, c * 128:(c + 1) * 128],
                        rhs=w[:, k, 1536 + h * 128:1536 + (h + 1) * 128], start=(k == 0), stop=(k == 7)),
                        [w, xgT], [ps], signal=(cc == 3 and k == 7))
            ot = ogt[c4 % 2]
            P.op("act", lambda e, ps=ps, ot=ot: e.activation(out=ot[:], in_=ps[:], func=AF.Exp, scale=-1.0), [ps], [ot])
            P.op("dve", lambda e, ot=ot: e.tensor_scalar(out=ot[:], in0=ot[:], scalar1=1.0, scalar2=None, op0=ALU.add),
                 [ot], [ot])
            P.op("dve", lambda e, ot=ot: e.reciprocal(out=ot[:], in_=ot[:]), [ot], [ot])
            P.op("dve", lambda e, ot=ot, c4=c4, h=h: e.tensor_tensor(
                out=og[:, c4 * 4:(c4 + 1) * 4, :], in0=ot[:].rearrange("p (a b) -> p a b", a=4),
                in1=hg[:, h * 128:(h + 1) * 128].unsqueeze(1).to_broadcast([128, 4, 128]), op=ALU.mult),
                [ot, hg], [og])
        for c in range(NC_):
            cs = slice(c * 128, (c + 1) * 128)
            euc = eu[:, c, h:h + 1]
            scc = sc[:, c, h:h + 1]
            kt = ktb[c % 2]
            P.op("pe", lambda e, kt=kt, cs=cs: e.transpose(out=kt[:], in_=kT[:, cs], identity=C.ident[:]),
                 [kT, C.ident], [kt])
            ks = kse[c % 2]
            P.op("act", lambda e, ks=ks, kt=kt, euc=euc: e.activation(out=ks[:], in_=kt[:], func=AF.Copy, scale=euc),
                 [kt, eu], [ks])
            S = pb[nps % 4]
            nps += 1
            P.op("pe", lambda e, S=S, cs=cs: e.matmul(S[:, 0:128], lhsT=kT[:, cs], rhs=qT[:, cs], start=True, stop=True),
                 [kT, qT], [S])
            p_ = pT[c % 2]
            P.op("dve", lambda e, p_=p_, S=S, euc=euc: e.scalar_tensor_tensor(
                out=p_[:], in0=S[:, 0:128], scalar=euc, in1=mle[:], op0=ALU.mult, op1=ALU.mult), [S, eu, mle], [p_])
            ha = hab[c % 2]
            P.op("pe", lambda e, ha=ha, p_=p_, c=c: e.matmul(ha[:], lhsT=p_[:], rhs=vaug[:, c, :], start=True,
                                                             stop=(c == 0)), [p_, vaug], [ha], signal=(c == 0))
            if c > 0:
                ab = Ab[c % 2]
                P.op("pe", lambda e, ha=ha, ab=ab, cs=cs: e.matmul(ha[:], lhsT=qT[:, cs], rhs=ab[:], start=False,
                                                                   stop=True), [qT, ab], [ha])
            if c < 31:
                dps = pb[nps % 4]
                nps += 1
                P.op("pe", lambda e, dps=dps, ks=ks, c=c: e.matmul(dps[:, 0:129], lhsT=ks[:], rhs=vaug[:, c, :],
                                                                   start=True, stop=True), [ks, vaug], [dps])
                if c == 0:
                    P.op("dve", lambda e, dps=dps: e.tensor_copy(out=Bst[:], in_=dps[:, 0:129]), [dps], [Bst])
                else:
                    egp = eg[:, c - 1, h:h + 1]
                    P.op("dve", lambda e, dps=dps, egp=egp: e.scalar_tensor_tensor(
                        out=Bst[:], in0=Bst[:], scalar=egp, in1=dps[:, 0:129], op0=ALU.mult, op1=ALU.add),
                        [Bst, eg, dps], [Bst])
                abn = Ab[(c + 1) % 2]
                egc = eg[:, c, h:h + 1]
                P.op("dve", lambda e, abn=abn, egc=egc: e.tensor_scalar(
                    out=abn[:], in0=Bst[:], scalar1=egc, scalar2=None, op0=ALU.mult), [Bst, eg], [abn])
            s_ = sm[c % 2]
            hd = hid[c % 2]
            P.op("act", lambda e, s_=s_, ha=ha, scc=scc: e.activation(
                out=s_[:, 0:1], in_=ha[:, 128:129], func=AF.Abs, scale=scc), [ha, sc], [s_])
            P.op("dve", lambda e, s_=s_: e.tensor_scalar(out=s_[:, 1:2], in0=s_[:, 0:1], scalar1=1.0, scalar2=None,
                                                         op0=ALU.max), [s_], [s_])
            P.op("dve", lambda e, s_=s_: e.reciprocal(out=s_[:, 2:3], in_=s_[:, 1:2]), [s_], [s_])
            P.op("dve", lambda e, s_=s_, scc=scc: e.tensor_scalar(out=s_[:, 3:4], in0=s_[:, 2:3], scalar1=scc,
                                                                   scalar2=None, op0=ALU.mult), [s_, sc], [s_])
            P.op("dve", lambda e, s_=s_, hd=hd, ha=ha: e.tensor_scalar(
                out=hd[:], in0=ha[:, 0:128], scalar1=s_[:, 3:4], scalar2=None, op0=ALU.mult), [ha, s_], [hd])
            ho = hob[c % 2]
            P.op("act", lambda e, s_=s_, hd=hd, ho=ho: e.activation(out=ho[:], in_=hd[:], func=AF.Square,
                                                                    accum_out=s_[:, 4:5]), [hd], [s_, ho])
            P.op("act", lambda e, s_=s_: e.activation(out=s_[:, 5:6], in_=s_[:, 4:5], func=AF.Ln, scale=1.0 / 128,
                                                      bias=epsc[:, 0:1]), [s_, epsc], [s_])
            P.op("act", lambda e, s_=s_: e.activation(out=s_[:, 6:7], in_=s_[:, 5:6], func=AF.Exp, scale=-0.5),
                 [s_], [s_])
            ho = hob[c % 2]
            P.op("dve", lambda e, s_=s_, hd=hd, ho=ho, c=c: e.scalar_tensor_tensor(
                out=ho[:], in0=hd[:], scalar=s_[:, 6:7], in1=og[:, c, :], op0=ALU.mult, op1=ALU.mult),
                [hd, s_, og], [ho])
            kt2 = ktb[(c + 1) % 2]
            P.op("pe", lambda e, kt2=kt2, ho=ho: e.transpose(out=kt2[:], in_=ho[:], identity=C.ident[:]),
                 [ho, C.ident], [kt2])
            hs = hst[c % 2]
            P.op("act", lambda e, hs=hs, kt2=kt2: e.copy(out=hs[:], in_=kt2[:]), [kt2], [hs])
            P.dma("sp", [(mo[c // 16, :, h, (c % 16) * 128:(c % 16 + 1) * 128], hs[:])], hs, write=False)
    P.release(m0)


def build_phaseD():
    nc = bass.Bass("TRN2", target_bir_lowering=False)
    dram = {}
    common_inputs(nc, dram, {
        "ident": ([128, 128], F32), "mle": ([128, 128], F32),
        "xg": ([2, 128, 8, TOWN], BF16), "wm": ([D, 2048], F32), "wgt": ([D, 8], F32),
        "bg": ([1, 8], F32), "cw": ([128, 4, 2, 4], F32), "hg": ([1, 512], F32),
    })
    dram["mo2"] = nc.dram_tensor("mo2", [2, 128, 4, TOWN], BF16, kind="ExternalOutput").ap()
    P = Prog(nc)
    C = Ctx()
    load_consts(P, C, dram)
    mlstm_phase(P, C, dram)
    P.barrier()
    P.emit()
    return nc


def host_mlstm(mlstm_w_in, b_gates, conv_w, head_g, p):
    w = np.asarray(mlstm_w_in, np.float32)
    hs = slice(p * 512, (p + 1) * 512)
    wm = np.concatenate([w[:, 0:1024][:, hs], w[:, 1024:2048][:, hs], w[:, 2048:3072][:, hs],
                         w[:, 3072:4096][:, hs]], axis=1)
    gi = w[:, 4096:4104][:, p * 4:(p + 1) * 4]
    gf = w[:, 4104:4112][:, p * 4:(p + 1) * 4]
    wgt = np.concatenate([gi, gf], axis=1)
    b = np.asarray(b_gates, np.float32)
    bg = np.concatenate([b[0:8][p * 4:(p + 1) * 4], b[8:16][p * 4:(p + 1) * 4]])[None, :]
    cwf = np.asarray(conv_w, np.float32)
    cw = np.zeros((128, 4, 2, 4), np.float32)
    for hl in range(4):
        for qk in range(2):
            ch0 = qk * 1024 + (4 * p + hl) * 128
            cw[:, hl, qk, :] = cwf[:, ch0:ch0 + 128].T
    hg = np.asarray(head_g, np.float32)[None, hs]
    a = np.arange(128)
    mle = (a[:, None] <= a[None, :]).astype(np.float32)
    return (np.ascontiguousarray(wm), np.ascontiguousarray(wgt), np.ascontiguousarray(bg), cw,
            np.ascontiguousarray(hg), mle)


def build_phaseC():
    nc = bass.Bass("TRN2", target_bir_lowering=False)
    dram = {}
    common_inputs(nc, dram, {
        "ident": ([128, 128], F32), "gcol": ([128, 96], F32), "grow": ([12, D], F32),
        "x": ([TOWN, D], F32), "mi": ([2, 64, 8, TOWN], BF16), "wo": ([D + 1, D], F32),
        "wg1": ([D + 1, DFF], F32), "wu1": ([D + 1, DFF], F32), "wd1": ([DFF + 1, D], F32),
        "wg2": ([D + 1, DFF], F32), "wu2": ([D + 1, DFF], F32), "wd2": ([DFF + 1, D], F32),
    })
    dram["xs"] = nc.dram_tensor("xs", [TOWN, D], F32, kind="Internal").ap()
    dram["xo"] = nc.dram_tensor("xo", [TOWN, D], F32, kind="ExternalOutput").ap()
    dram["xg"] = nc.dram_tensor("xg", [128, 8, TOWN], BF16, kind="ExternalOutput").ap()
    P = Prog(nc)
    C = Ctx()
    load_consts(P, C, dram)
    C.gcol = P.alloc("gcol", [128, 96], F32)
    P.dma("sp", [(C.gcol[:], dram["gcol"])], C.gcol)
    ffn_phase(P, C, dram, ("wg1", "wu1", "wd1"), 4, dram["x"], dram["xs"], mixer=(dram["mi"], "wo", 8, 3))
    ffn_phase(P, C, dram, ("wg2", "wu2", "wd2"), 6, dram["xs"], dram["xo"], xg_out=dram["xg"], g2idx=8)
    P.barrier()
    P.emit()
    return nc


def build_phaseE():
    nc = bass.Bass("TRN2", target_bir_lowering=False)
    dram = {}
    common_inputs(nc, dram, {
        "ident": ([128, 128], F32), "gcol": ([128, 96], F32), "grow": ([12, D], F32),
        "x": ([TOWN, D], F32), "mi": ([2, 128, 4, TOWN], BF16), "wo": ([D + 1, D], F32),
        "wg1": ([D + 1, DFF], F32), "wu1": ([D + 1, DFF], F32), "wd1": ([DFF + 1, D], F32),
    })
    dram["xo"] = nc.dram_tensor("xo", [TOWN, D], F32, kind="ExternalOutput").ap()
    P = Prog(nc)
    C = Ctx()
    load_consts(P, C, dram)
    C.gcol = P.alloc("gcol", [128, 96], F32)
    P.dma("sp", [(C.gcol[:], dram["gcol"])], C.gcol)
    ffn_phase(P, C, dram, ("wg1", "wu1", "wd1"), 10, dram["x"], dram["xo"], mixer=(dram["mi"], "wo", 4, 9))
    P.barrier()
    P.emit()
    return nc


def host_wo_attn(attn_w_out):
    w = np.asarray(attn_w_out, np.float32)
    rows = []
    for r in range(2):
        for j in range(4):
            r0 = 256 * r + 128 * j if j < 2 else 512 + 256 * r + 128 * (j - 2)
            rows.append(w[r0:r0 + 128])
    return np.ascontiguousarray(np.concatenate(rows, axis=0))


def _pad(a, c):
    a = np.asarray(a, np.float32)
    return np.concatenate([a, np.full((1, a.shape[1]), float(c), np.float32)], axis=0)


def _run(nc, ins):
    res = run_bass_kernel_spmd(nc, ins, core_ids=list(range(len(ins))))
    return res.results


def kernel(x, norm_g, ffn_w_gate, ffn_w_up, ffn_w_down, attn_w_in, attn_w_out, rel_bias,
           mlstm_w_in, mlstm_b_gates, mlstm_conv_w, mlstm_head_g, mlstm_w_out):
    x = np.asarray(x, np.float32)
    fg = np.asarray(ffn_w_gate, np.float32)
    fu = np.asarray(ffn_w_up, np.float32)
    fd = np.asarray(ffn_w_down, np.float32)
    ident, gcol, grow = host_consts(norm_g)
    import os
    NCORE = int(os.environ.get("KNCORE", "8"))
    base = {"ident": ident, "gcol": gcol, "grow": grow}
    ins = []
    for c in range(NCORE):
        b, p = c // 2, c % 2
        ins.append(dict(base, x=np.ascontiguousarray(x[b, p * TOWN:(p + 1) * TOWN]),
                        wg=_pad(fg[0, 0], c), wu=_pad(fu[0, 0], c), wd=_pad(fd[0, 0], c)))
    rA = _run(build_phaseA(), ins)
    ins = []
    for c in range(NCORE):
        b, p = c // 2, c % 2
        dmask, tri, bt = host_attn_consts(rel_bias, list(range(4 * p, 4 * p + 4)))
        xg = np.ascontiguousarray(np.stack([np.asarray(rA[2 * b]["xg"]), np.asarray(rA[2 * b + 1]["xg"])]))
        ins.append({"ident": ident, "dmask": dmask, "tri": tri, "xg": xg,
                    "wqkv": host_wqkv(np.asarray(attn_w_in)[0], p), "bt": bt})
    rB = _run(build_phaseB(), ins)
    ins = []
    wo0 = host_wo_attn(np.asarray(attn_w_out)[0])
    for c in range(NCORE):
        b, p = c // 2, c % 2
        mi = np.ascontiguousarray(np.stack([np.asarray(rB[2 * b]["mo"])[p], np.asarray(rB[2 * b + 1]["mo"])[p]]))
        ins.append(dict(base, x=np.asarray(rA[c]["xo"]), mi=mi, wo=_pad(wo0, c),
                        wg1=_pad(fg[0, 1], c), wu1=_pad(fu[0, 1], c), wd1=_pad(fd[0, 1], c),
                        wg2=_pad(fg[1, 0], c), wu2=_pad(fu[1, 0], c), wd2=_pad(fd[1, 0], c)))
    rC = _run(build_phaseC(), ins)
    ins = []
    for c in range(NCORE):
        b, p = c // 2, c % 2
        wm, wgt, bg, cw, hg, mle = host_mlstm(np.asarray(mlstm_w_in)[0], np.asarray(mlstm_b_gates)[0],
                                              np.asarray(mlstm_conv_w)[0], np.asarray(mlstm_head_g)[0], p)
        xg = np.ascontiguousarray(np.stack([np.asarray(rC[2 * b]["xg"]), np.asarray(rC[2 * b + 1]["xg"])]))
        ins.append({"ident": ident, "mle": mle, "xg": xg, "wm": wm, "wgt": wgt, "bg": bg, "cw": cw, "hg": hg})
    rD = _run(build_phaseD(), ins)
    ins = []
    wo1 = np.ascontiguousarray(np.asarray(mlstm_w_out, np.float32)[0])
    for c in range(NCORE):
        b, p = c // 2, c % 2
        mi = np.ascontiguousarray(np.stack([np.asarray(rD[2 * b]["mo2"])[p], np.asarray(rD[2 * b + 1]["mo2"])[p]]))
        ins.append(dict(base, x=np.asarray(rC[c]["xo"]), mi=mi, wo=_pad(wo1, c), wg1=_pad(fg[1, 1], c),
                        wu1=_pad(fu[1, 1], c), wd1=_pad(fd[1, 1], c)))
    rE = _run(build_phaseE(), ins)
    out = np.zeros((4, T, D), np.float32)
    for c in range(NCORE):
        b, p = c // 2, c % 2
        out[b, p * TOWN:(p + 1) * TOWN] = np.asarray(rE[c]["xo"])
    return out
```
